# Optimizing a Trainium2 kernel written in Bass

```python
import jax, jax.numpy as jnp
from jax import lax
import numpy as np

D_MODEL = 2048
BATCH = 4
SEQ = 8192
DEPTH = 1
DEC_BATCH = 8
DEC_SEQ = 32
PAST_LEN = 1024

CHUNK = 64
D_RNN = D_MODEL
RNN_BLOCKS = 16
RNN_BLOCK_DIM = D_RNN // RNN_BLOCKS
CONV_W = 4
RG_C = 8.0
N_HEADS = 16
HEAD_DIM = D_MODEL // N_HEADS
D_ATT = N_HEADS * HEAD_DIM
LEFT_CHUNKS = 8
BAND = (LEFT_CHUNKS + 1) * CHUNK
ATT_REACH = LEFT_CHUNKS * CHUNK
MAX_REL = 128
REL_BUCKETS = MAX_REL + CHUNK
N_KEYS = 128
N_EXPERTS = N_KEYS * N_KEYS
PEER_HEADS = 8
D_KEY = 256
PEER_TOPK = 16
PEER_BLOCK = 128
D_IN = 2 * D_RNN + 3 * D_ATT + 2 * D_MODEL
SPLIT_POINTS = (D_RNN, 2 * D_RNN, 2 * D_RNN + D_ATT, 2 * D_RNN + 2 * D_ATT, 2 * D_RNN + 3 * D_ATT, 2 * D_RNN + 3 * D_ATT + D_MODEL)
ALPHA = (2.0 * DEPTH) ** 0.25
BETA = (8.0 * DEPTH) ** -0.25
LN_EPS = 1e-5
NEG_INF = -1e30

kernel_name = 'hybrid_streaming_encoder_step'


def layer_norm(x):
    x32 = x.astype(jnp.float32)
    mu = jnp.mean(x32, axis=-1, keepdims=True)
    var = jnp.mean(jnp.square(x32 - mu), axis=-1, keepdims=True)
    return ((x32 - mu) * lax.rsqrt(var + LN_EPS)).astype(x.dtype)


def affine_layer_norm(x, g, b):
    return layer_norm(x) * g + b


def adaln_modulation(c, w_ada, b_ada):
    mod = (jax.nn.silu(c) @ w_ada + b_ada)[:, None, :]
    return jnp.split(mod, 6, axis=-1)


def causal_depthwise_conv(x, buf, w, b):
    t = x.shape[1]
    x_ext = jnp.concatenate([buf.astype(x.dtype), x], axis=1)
    y = b + w[0] * x_ext[:, 0:t]
    for j in range(1, CONV_W):
        y = y + w[j] * x_ext[:, j:j + t]
    return y, x_ext[:, -(CONV_W - 1):]


def block_diag_linear(x, w, b):
    bsz, t, _ = x.shape
    xb = x.reshape(bsz, t, RNN_BLOCKS, RNN_BLOCK_DIM)
    return jnp.einsum('btni,nio->btno', xb, w).reshape(bsz, t, D_RNN) + b


def rg_lru(xc, h0, w_ra, b_ra, w_ri, b_ri, lam):
    x32 = xc.astype(jnp.float32)
    r = jax.nn.sigmoid(block_diag_linear(xc, w_ra, b_ra).astype(jnp.float32))
    i = jax.nn.sigmoid(block_diag_linear(xc, w_ri, b_ri).astype(jnp.float32))
    log_a = -RG_C * r * jax.nn.softplus(-lam.astype(jnp.float32))
    a = jnp.exp(log_a)
    u = jnp.sqrt(-jnp.expm1(2.0 * log_a)) * (i * x32)

    def step(h, au):
        h = au[0] * h + au[1]
        return h, h

    h_last, hs = lax.scan(step, h0.astype(jnp.float32), (jnp.swapaxes(a, 0, 1), jnp.swapaxes(u, 0, 1)))
    return jnp.swapaxes(hs, 0, 1).astype(xc.dtype), h_last.astype(xc.dtype)


def band_attend(q, k, v, q_pos, k_pos, rel_bias):
    s = jnp.einsum('bqhd,bkhd->bhqk', q, k).astype(jnp.float32) * (HEAD_DIM ** -0.5)
    dist = q_pos[:, None] - k_pos[None, :]
    s = s + rel_bias[:, jnp.clip(dist, -(CHUNK - 1), MAX_REL) + (CHUNK - 1)].astype(jnp.float32)
    qc = q_pos[:, None] // CHUNK
    kc = k_pos[None, :] // CHUNK
    visible = (k_pos[None, :] >= 0) & (kc <= qc) & (kc >= qc - LEFT_CHUNKS)
    p = jax.nn.softmax(jnp.where(visible, s, NEG_INF), axis=-1).astype(v.dtype)
    return jnp.einsum('bhqk,bkhd->bqhd', p, v)


def prompt_band_attention(q, k, v, rel_bias):
    bsz, t, _, _ = q.shape
    n_chunks = t // CHUNK
    pad = ((0, 0), (ATT_REACH, 0), (0, 0), (0, 0))
    k_pad, v_pad = jnp.pad(k, pad), jnp.pad(v, pad)
    q_chunks = jnp.moveaxis(q.reshape(bsz, n_chunks, CHUNK, N_HEADS, HEAD_DIM), 1, 0)

    def one_chunk(args):
        idx, q_blk = args
        start = idx * CHUNK
        k_blk = lax.dynamic_slice_in_dim(k_pad, start, BAND, axis=1)
        v_blk = lax.dynamic_slice_in_dim(v_pad, start, BAND, axis=1)
        q_pos = start + jnp.arange(CHUNK)
        k_pos = start - ATT_REACH + jnp.arange(BAND)
        return band_attend(q_blk, k_blk, v_blk, q_pos, k_pos, rel_bias)

    out = lax.map(one_chunk, (jnp.arange(n_chunks), q_chunks))
    return jnp.moveaxis(out, 0, 1).reshape(bsz, t, D_ATT)


def make_sample_attention(cache_k, cache_v):
    def attend(q, k, v, rel_bias):
        bsz, t, _, _ = q.shape
        n_past = cache_k.shape[1]
        k_all = jnp.concatenate([cache_k.astype(k.dtype), k], axis=1)
        v_all = jnp.concatenate([cache_v.astype(v.dtype), v], axis=1)
        q_pos = PAST_LEN + jnp.arange(t)
        k_pos = PAST_LEN - n_past + jnp.arange(n_past + t)
        return band_attend(q, k_all, v_all, q_pos, k_pos, rel_bias).reshape(bsz, t, D_ATT)
    return attend


def peer_ffn(h, w_pq, peer_keys, peer_u, peer_v):
    bsz, t, d = h.shape
    n_tok = bsz * t
    n_blocks = -(-n_tok // PEER_BLOCK)
    flat = jnp.pad(h.reshape(n_tok, d), ((0, n_blocks * PEER_BLOCK - n_tok), (0, 0)))

    def one_block(xb):
        q = (xb @ w_pq).reshape(PEER_BLOCK, PEER_HEADS, 2, D_KEY // 2)
        s = jnp.einsum('nhpd,hpkd->nhpk', q, peer_keys).astype(jnp.float32)
        s1, i1 = lax.top_k(s[:, :, 0], PEER_TOPK)
        s2, i2 = lax.top_k(s[:, :, 1], PEER_TOPK)
        cand_s = (s1[..., :, None] + s2[..., None, :]).reshape(PEER_BLOCK, PEER_HEADS, PEER_TOPK * PEER_TOPK)
        cand_i = (i1[..., :, None] * N_KEYS + i2[..., None, :]).reshape(PEER_BLOCK, PEER_HEADS, PEER_TOPK * PEER_TOPK)
        top_s, pos = lax.top_k(cand_s, PEER_TOPK)
        expert = jnp.take_along_axis(cand_i, pos, axis=-1)
        g = jax.nn.softmax(top_s, axis=-1).astype(xb.dtype)
        act = jax.nn.gelu(jnp.einsum('nhkd,nd->nhk', jnp.take(peer_u, expert, axis=0), xb), approximate=False)
        return jnp.einsum('nhk,nhkd->nd', g * act, jnp.take(peer_v, expert, axis=0))

    out = lax.map(one_block, flat.reshape(n_blocks, PEER_BLOCK, d))
    return out.reshape(n_blocks * PEER_BLOCK, d)[:n_tok].reshape(bsz, t, d)


def encoder_layer(x, c, conv_buf, rnn_h0, attend, w_ada, b_ada, w_in, conv_w, conv_b, w_ra, b_ra, w_ri, b_ri, rg_lambda, rel_bias, w_out, ln1_g, ln1_b, w_pq, peer_keys, peer_u, peer_v, ln2_g, ln2_b):
    bsz, t, _ = x.shape
    shift1, scale1, gate1, shift2, scale2, gate2 = adaln_modulation(c, w_ada, b_ada)
    h = layer_norm(x) * (1.0 + scale1) + shift1
    xr, yr, q, k, v, ga, gb = jnp.split(h @ w_in, SPLIT_POINTS, axis=-1)
    xc, conv_state = causal_depthwise_conv(xr, conv_buf, conv_w, conv_b)
    hr, rnn_state = rg_lru(xc, rnn_h0, w_ra, b_ra, w_ri, b_ri, rg_lambda)
    out_a = hr * jax.nn.gelu(yr, approximate=False)
    heads = (bsz, t, N_HEADS, HEAD_DIM)
    k, v = k.reshape(heads), v.reshape(heads)
    out_b = attend(q.reshape(heads), k, v, rel_bias)
    mixed = (jax.nn.sigmoid(ga) * out_a + jax.nn.sigmoid(gb) * out_b) @ w_out
    x = affine_layer_norm(ALPHA * x + gate1 * mixed, ln1_g, ln1_b)
    h2 = layer_norm(x) * (1.0 + scale2) + shift2
    x = affine_layer_norm(ALPHA * x + gate2 * peer_ffn(h2, w_pq, peer_keys, peer_u, peer_v), ln2_g, ln2_b)
    return x, conv_state, rnn_state, k, v


def setup_inputs(seed: int = 0) -> dict:
    key = jax.random.key(seed)
    ks = jax.random.split(key, 28)
    f32 = jnp.float32

    def nrm(k, shape, scale):
        return jax.random.normal(k, shape, f32) * scale

    att_cache = min(ATT_REACH, PAST_LEN)
    a = jax.random.uniform(ks[17], (DEPTH, D_RNN), f32, 0.9, 0.999) ** (1.0 / RG_C)
    return {
        'x_prompt': nrm(ks[0], (BATCH, SEQ, D_MODEL), 1.0),
        'x_sample': nrm(ks[1], (DEC_BATCH, DEC_SEQ, D_MODEL), 1.0),
        'c_prompt': nrm(ks[2], (BATCH, D_MODEL), 1.0),
        'c_sample': nrm(ks[3], (DEC_BATCH, D_MODEL), 1.0),
        'state_conv': nrm(ks[4], (DEPTH, DEC_BATCH, CONV_W - 1, D_RNN), 1.0),
        'state_rnn': nrm(ks[5], (DEPTH, DEC_BATCH, D_RNN), 1.0),
        'cache_k': nrm(ks[6], (DEPTH, DEC_BATCH, att_cache, N_HEADS, HEAD_DIM), 1.0),
        'cache_v': nrm(ks[7], (DEPTH, DEC_BATCH, att_cache, N_HEADS, HEAD_DIM), 1.0),
        'w_ada': nrm(ks[8], (DEPTH, D_MODEL, 6 * D_MODEL), 0.5 * D_MODEL ** -0.5),
        'b_ada': nrm(ks[9], (DEPTH, 6 * D_MODEL), 0.01),
        'w_in': nrm(ks[10], (DEPTH, D_MODEL, D_IN), D_MODEL ** -0.5),
        'conv_w': nrm(ks[11], (DEPTH, CONV_W, D_RNN), CONV_W ** -0.5),
        'conv_b': nrm(ks[12], (DEPTH, D_RNN), 0.01),
        'w_ra': nrm(ks[13], (DEPTH, RNN_BLOCKS, RNN_BLOCK_DIM, RNN_BLOCK_DIM), RNN_BLOCK_DIM ** -0.5),
        'b_ra': nrm(ks[14], (DEPTH, D_RNN), 0.01),
        'w_ri': nrm(ks[15], (DEPTH, RNN_BLOCKS, RNN_BLOCK_DIM, RNN_BLOCK_DIM), RNN_BLOCK_DIM ** -0.5),
        'b_ri': nrm(ks[16], (DEPTH, D_RNN), 0.01),
        'rg_lambda': jnp.log(a) - jnp.log1p(-a),
        'rel_bias': nrm(ks[18], (DEPTH, N_HEADS, REL_BUCKETS), 0.1),
        'w_out': nrm(ks[19], (DEPTH, D_MODEL, D_MODEL), BETA * D_MODEL ** -0.5),
        'ln1_g': 1.0 + nrm(ks[20], (DEPTH, D_MODEL), 0.01),
        'ln1_b': nrm(ks[21], (DEPTH, D_MODEL), 0.01),
        'w_pq': nrm(ks[22], (DEPTH, D_MODEL, PEER_HEADS * D_KEY), D_MODEL ** -0.5),
        'peer_keys': nrm(ks[23], (DEPTH, PEER_HEADS, 2, N_KEYS, D_KEY // 2), (D_KEY // 2) ** -0.5),
        'peer_u': nrm(ks[24], (DEPTH, N_EXPERTS, D_MODEL), D_MODEL ** -0.5),
        'peer_v': nrm(ks[25], (DEPTH, N_EXPERTS, D_MODEL), BETA * PEER_HEADS ** -0.5),
        'ln2_g': 1.0 + nrm(ks[26], (DEPTH, D_MODEL), 0.01),
        'ln2_b': nrm(ks[27], (DEPTH, D_MODEL), 0.01),
    }


def reference(x_prompt, x_sample, c_prompt, c_sample, state_conv, state_rnn, cache_k, cache_v, w_ada, b_ada, w_in, conv_w, conv_b, w_ra, b_ra, w_ri, b_ri, rg_lambda, rel_bias, w_out, ln1_g, ln1_b, w_pq, peer_keys, peer_u, peer_v, ln2_g, ln2_b):
    y_prompt, y_sample = x_prompt, x_sample
    p_conv, p_rnn, p_k, p_v = [], [], [], []
    s_conv, s_rnn, s_k, s_v = [], [], [], []
    bsz_p, t_p = x_prompt.shape[0], x_prompt.shape[1]
    keep = min(ATT_REACH, t_p)
    for l in range(DEPTH):
        w = (w_ada[l], b_ada[l], w_in[l], conv_w[l], conv_b[l], w_ra[l], b_ra[l], w_ri[l], b_ri[l], rg_lambda[l], rel_bias[l], w_out[l], ln1_g[l], ln1_b[l], w_pq[l], peer_keys[l], peer_u[l], peer_v[l], ln2_g[l], ln2_b[l])
        zero_conv = jnp.zeros((bsz_p, CONV_W - 1, D_RNN), x_prompt.dtype)
        zero_h = jnp.zeros((bsz_p, D_RNN), jnp.float32)
        y_prompt, cs, hs, ks_, vs_ = encoder_layer(y_prompt, c_prompt, zero_conv, zero_h, prompt_band_attention, *w)
        p_conv.append(cs)
        p_rnn.append(hs)
        p_k.append(ks_[:, -keep:])
        p_v.append(vs_[:, -keep:])
        y_sample, cs, hs, ks_, vs_ = encoder_layer(y_sample, c_sample, state_conv[l], state_rnn[l], make_sample_attention(cache_k[l], cache_v[l]), *w)
        s_conv.append(cs)
        s_rnn.append(hs)
        s_k.append(ks_)
        s_v.append(vs_)
    return (y_prompt, y_sample, jnp.stack(p_conv), jnp.stack(p_rnn), jnp.stack(p_k), jnp.stack(p_v), jnp.stack(s_conv), jnp.stack(s_rnn), jnp.stack(s_k), jnp.stack(s_v))
```

```python
import numpy as np
from contextlib import ExitStack
import concourse.bass as bass
import concourse.mybir as mybir
from concourse.bass_utils import run_bass_kernel_spmd

F32 = mybir.dt.float32
BF16 = mybir.dt.bfloat16
I32 = mybir.dt.int32
U32 = mybir.dt.uint32
ALU = mybir.AluOpType
AF = mybir.ActivationFunctionType
AX = mybir.AxisListType

D = 2048
NCH = 16
D_IN = 14336
ALPHA = 2.0 ** 0.25
LN_EPS = 1e-5
NEG = -1e30
N_EXP = 16384
NBLK = 144


class Sched:
    def __init__(self, nc):
        self.nc = nc
        self.eng = {"pe": nc.tensor, "act": nc.scalar, "dve": nc.vector, "pool": nc.gpsimd, "sp": nc.sync}
        self.sem, self.cnt, self._cms = {}, {}, []
        for e in self.eng:
            self.sem[e] = self._newsem("e_" + e)
            self.cnt[e] = 0
        self.lane_sem, self.lane_cnt = {}, {}
        self.last_w, self.readers, self.subs = {}, {}, {}
        self.seen = {e: {} for e in self.eng}
        self.n_inst = 0

    def _newsem(self, name):
        cm = self.nc.semaphore(name)
        h = cm.__enter__()
        self._cms.append(cm)
        return h

    def close(self):
        for cm in reversed(self._cms):
            cm.__exit__(None, None, None)

    def _semh(self, sk):
        return self.sem[sk] if sk in self.sem else self.lane_sem[sk]

    @staticmethod
    def _norm(k):
        return k if isinstance(k, tuple) else (k, None)

    def _wkeys(self, k):
        b, s = self._norm(k)
        if s is None:
            return [(b, None)] + [(b, x) for x in self.subs.get(b, ())]
        return [(b, None), (b, s)]

    def _deps(self, reads, writes):
        deps = []
        for k in reads:
            for kk in self._wkeys(k):
                if kk in self.last_w:
                    deps.append(self.last_w[kk])
        for k in writes:
            for kk in self._wkeys(k):
                if kk in self.last_w:
                    deps.append(self.last_w[kk])
                deps.extend(self.readers.get(kk, ()))
        return deps

    def _wait(self, e, deps, skip_self=False):
        best = {}
        for sk, v in deps:
            if skip_self and sk == e:
                continue
            if v > best.get(sk, 0):
                best[sk] = v
        for sk, v in best.items():
            if self.seen[e].get(sk, 0) >= v:
                continue
            self.eng[e].wait_ge(self._semh(sk), v)
            self.seen[e][sk] = v

    def _commit(self, sp, reads, writes):
        wn = [self._norm(k) for k in writes]
        for (b, s) in wn:
            if s is not None:
                self.subs.setdefault(b, set()).add(s)
                self.last_w[(b, s)] = sp
                self.readers[(b, s)] = []
            else:
                for kk in self._wkeys(b):
                    self.last_w[kk] = sp
                    self.readers[kk] = []
        for k in reads:
            kn = self._norm(k)
            if kn in wn:
                continue
            if kn[1] is not None:
                self.subs.setdefault(kn[0], set()).add(kn[1])
            self.readers.setdefault(kn, []).append(sp)

    def op(self, e, fn, reads=(), writes=()):
        self._wait(e, self._deps(reads, writes), skip_self=(e == "pe"))
        ins = fn(self.eng[e])
        self.cnt[e] += 1
        ins.then_inc(self.sem[e], 1)
        self._commit((e, self.cnt[e]), reads, writes)
        self.n_inst += 1

    def dma(self, q, lane, fn, reads=(), writes=()):
        if lane not in self.lane_sem:
            self.lane_sem[lane] = self._newsem("l_%d" % len(self.lane_sem))
            self.lane_cnt[lane] = 0
        self._wait(q, self._deps(reads, writes))
        ins = fn(self.eng[q])
        self.lane_cnt[lane] += 16
        ins.then_inc(self.lane_sem[lane], 16)
        self._commit((lane, self.lane_cnt[lane]), reads, writes)
        self.n_inst += 1

    def _all_pts(self):
        pts = [(x, self.cnt[x]) for x in self.eng if self.cnt[x] > 0]
        pts += [(l, c) for l, c in self.lane_cnt.items() if c > 0]
        return pts

    def barrier(self):
        pts = self._all_pts()
        for e in self.eng:
            self._wait(e, pts)

    def finish(self, e="sp"):
        self._wait(e, self._all_pts())


import os
DEBUG_STOP = int(os.environ.get('KSTOP', '0'))


class _Stop(Exception):
    pass


def _chk(n):
    if DEBUG_STOP == n:
        raise _Stop()


def keys(name, n):
    return [(name, i) for i in range(n)]


def build_nc(NT_PRE, NT_MAIN):
    NKV = 4
    nc = bass.Bass("TRN2", target_bir_lowering=False)

    def din(name, shape, dt=F32):
        return nc.dram_tensor(name, shape, dt, kind="ExternalInput").ap()

    def dout(name, shape, dt=F32):
        return nc.dram_tensor(name, shape, dt, kind="ExternalOutput").ap()

    x_main = din("x_main", [NT_MAIN * 128, D])
    x_pre = din("x_pre", [NT_PRE * 128, D])
    x_smp = din("x_smp", [32, D])
    cT = din("cT", [128, NCH, 2])
    cst_d = din("cst", [128, 8])
    st_conv = din("st_conv", [3, D])
    st_rnn = din("st_rnn", [D])
    ckT = din("ckT", [128, NCH, 512])
    cv = din("cv", [512, D])
    w_ada = din("w_ada", [D, 6 * D])
    badaT = din("badaT", [128, 96])
    w_in = din("w_in", [D, D_IN])
    colp_d = din("colp", [128, NCH, 8])
    wra_d = din("wra", [128, NCH, 128])
    wri_d = din("wri", [128, NCH, 128])
    BT_d = din("BT", [128, 5, NCH, 128])
    maskT_d = din("maskT", [128, 5, 128])
    w_out = din("w_out", [D, D])
    ln1g = din("ln1g", [1, D])
    ln1b = din("ln1b", [1, D])
    w_pq = din("w_pq", [D, D])
    pkT_d = din("pkT", [128, NCH, 128])
    peer_u = din("peer_u", [N_EXP, D])
    peer_v = din("peer_v", [N_EXP, D])
    ln2g = din("ln2g", [1, D])
    ln2b = din("ln2b", [1, D])

    y_main = dout("y_main", [NT_MAIN * 128, D])
    y_smp = dout("y_smp", [32, D])
    o_conv = dout("o_conv", [3, D])
    o_rnn = dout("o_rnn", [D])
    o_k = dout("o_k", [NKV * 128, D])
    o_v = dout("o_v", [NKV * 128, D])
    s_conv = dout("s_conv", [3, D])
    s_rnn = dout("s_rnn", [D])
    s_k = dout("s_k", [32, D])
    s_v = dout("s_v", [32, D])

    wbf_d = nc.dram_tensor("wbf_scr", [NBLK, 128, NCH, 128], BF16, kind="Internal").ap()
    modrow_d = nc.dram_tensor("modrow_scr", [2, 6 * D], F32, kind="Internal").ap()

    S = Sched(nc)
    es = ExitStack()
    bc_reg = nc.gpsimd.to_reg(N_EXP - 1)

    def sb(name, shape, dt=F32):
        return es.enter_context(nc.sbuf_tensor(name, shape, dt))

    def ps(name, shape, dt=F32):
        return es.enter_context(nc.psum_tensor(name, shape, dt))

    cst = sb("cst_s", [128, 8])
    identf = sb("identf", [128, 128])
    identb = sb("identb", [128, 128], BF16)
    onesb = sb("onesb", [128, 128], BF16)
    colp = sb("colp_s", [128, NCH, 8])
    coef = sb("coef", [128, NCH])
    cTs = sb("cTs", [128, NCH, 2])
    mcol = sb("mcol", [128, 96, 2])
    bada = sb("bada", [128, 96])
    hst = sb("hst", [128, NCH])
    halo = sb("halo", [128, NCH, 3])
    wra_b = sb("wra_b", [128, NCH, 128], BF16)
    wri_b = sb("wri_b", [128, NCH, 128], BF16)
    pk_b = sb("pk_b", [128, NCH, 128], BF16)
    BM = sb("BM", [128, NCH, 5, 128], BF16)
    Kr = sb("Kr", [128, NCH, 5, 128], BF16)
    Vr = sb("Vr", [128, 5, D], BF16)
    X = sb("X", [128, D])
    XE = sb("XE", [128, NCH, 131])
    T = [sb("T%d" % i, [128, D]) for i in range(7)]
    Bh = sb("Bh", [128, NCH, 128], BF16)
    Bx = sb("Bx", [128, D], BF16)
    Bq = sb("Bq", [128, NCH, 128], BF16)
    Bm = sb("Bm", [128, NCH, 128], BF16)
    Eb = [sb("E%d" % i, [128, 5, 128], BF16) for i in range(2)]
    NW = 4
    Wsl = [sb("W%d" % i, [128, NCH, 128], BF16) for i in range(NW)]
    NC_ = 2
    Csl = [sb("C%d" % i, [128, D]) for i in range(NC_)]
    stt = sb("stt", [128, 24])
    mv = sb("mv", [128, 2])
    sd = sb("sd", [128, 1])
    rstd = sb("rstd", [128, 1])
    RZ = sb("RZ", [128, 128])
    mk = sb("mk", [128, 128])
    st16 = sb("st16", [128, NCH, 16])
    ix16 = sb("ix16", [128, NCH, 16], U32)
    ixf = sb("ixf", [128, NCH, 16])
    i1s = sb("i1s", [128, 8, 16])
    tops = sb("tops", [128, 8, 16])
    eidf = sb("eidf", [128, 128])
    eid = sb("eid", [128, 128], I32)
    gw = sb("gw", [128, 8, 16])
    gs = sb("gs", [128, 8])
    pre = sb("pre", [128, 128])
    cf = sb("cf", [128, 128])

    PA = [ps("PA%d" % i, [128, 1024]) for i in range(2)]
    PSs = ps("PSs", [128, 1024])
    PV = ps("PV", [128, 512])
    PT = ps("PT", [128, 1024], BF16)

    T3 = [t[:].rearrange("p (n t) -> p n t", n=NCH) for t in T]

    def bc16(ap2):
        return ap2.unsqueeze(2).to_broadcast([128, NCH, 128])

    wstate = {"order": [], "issued": 0, "used": 0}

    def w_issue():
        i = wstate["issued"]
        if i >= len(wstate["order"]):
            return
        blk = wstate["order"][i]
        s = i % NW
        S.dma("sp", "l_W%d" % s, lambda e: e.dma_start(out=Wsl[s][:], in_=wbf_d[blk]),
              reads=[("wbf", blk)], writes=["W%d" % s])
        wstate["issued"] += 1

    def w_get():
        i = wstate["used"]
        while wstate["issued"] < min(i + NW, len(wstate["order"])):
            w_issue()
        wstate["used"] += 1
        s = i % NW
        return Wsl[s], "W%d" % s

    cstate = {"order": [], "issued": 0, "used": 0}

    def c_issue():
        i = cstate["issued"]
        if i >= len(cstate["order"]):
            return
        src = cstate["order"][i]
        s = i % NC_
        S.dma("sp", "l_C%d" % s, lambda e: e.dma_start(out=Csl[s][:], in_=src.partition_broadcast(128)),
              reads=["modrow"], writes=["C%d" % s])
        cstate["issued"] += 1

    def c_get():
        i = cstate["used"]
        while cstate["issued"] < min(i + NC_, len(cstate["order"])):
            c_issue()
        cstate["used"] += 1
        s = i % NC_
        return Csl[s], "C%d" % s

    S.dma("sp", "l_cst", lambda e: e.dma_start(out=cst[:], in_=cst_d), writes=["cst"])
    S.dma("sp", "l_colp", lambda e: e.dma_start(out=colp[:], in_=colp_d), writes=["colp"])
    S.dma("sp", "l_cT", lambda e: e.dma_start(out=cTs[:], in_=cT), writes=["cTs"])
    S.dma("sp", "l_bada", lambda e: e.dma_start(out=bada[:], in_=badaT), writes=["bada"])
    S.op("pool", lambda e: e.memset(identf[:], 1.0), writes=["identf"])
    S.op("pool", lambda e: e.affine_select(out=identf[:], in_=identf[:], pattern=[[-1, 128]], compare_op=ALU.is_equal,
                                           fill=0.0, base=0, channel_multiplier=1), reads=["identf"], writes=["identf"])
    S.op("dve", lambda e: e.tensor_copy(out=identb[:], in_=identf[:]), reads=["identf"], writes=["identb"])
    S.op("dve", lambda e: e.memset(onesb[:], 1.0), writes=["onesb"])
    S.op("act", lambda e: e.activation(out=coef[:], in_=colp[:, :, 7], func=AF.Exp, scale=-1.0), reads=["colp"], writes=["coef"])
    S.op("act", lambda e: e.activation(out=coef[:], in_=coef[:], func=AF.Ln, bias=cst[:, 5:6]), reads=["coef", "cst"], writes=["coef"])
    S.op("dve", lambda e: e.tensor_scalar(coef[:], coef[:], -8.0, None, ALU.mult), reads=["coef"], writes=["coef"])
    S.op("act", lambda e: e.activation(out=cTs[:], in_=cTs[:], func=AF.Silu), reads=["cTs"], writes=["cTs"])

    for j in range(96):
        t = T[j % 4]
        tk = "T%d" % (j % 4)
        S.dma("sp", "l_" + tk, lambda e: e.dma_start(
            out=t[:].rearrange("p (k c) -> p k c", k=NCH),
            in_=w_ada[:, j * 128:(j + 1) * 128].rearrange("(k p) c -> p k c", p=128)), writes=[tk])
        for k in range(NCH):
            S.op("pe", lambda e: e.matmul(PV[:, 2 * j:2 * j + 2], lhsT=t[:, k * 128:(k + 1) * 128], rhs=cTs[:, k, :],
                                          start=(k == 0), stop=(k == NCH - 1)), reads=[tk, "cTs"], writes=["PV"])
    S.op("dve", lambda e: e.tensor_tensor(out=mcol[:], in0=PV[:, 0:192].rearrange("p (j s) -> p j s", s=2),
                                          in1=bada[:].unsqueeze(2).to_broadcast([128, 96, 2]), op=ALU.add),
         reads=["PV", "bada"], writes=["mcol"])
    for lo in (16, 64):
        S.op("dve", lambda e: e.tensor_scalar(mcol[:, lo:lo + 16, :], mcol[:, lo:lo + 16, :], 1.0, None, ALU.add),
             reads=["mcol"], writes=["mcol"])
    for s in range(2):
        S.op("dve", lambda e: e.tensor_copy(out=T[4][:, 0:96], in_=mcol[:, :, s]), reads=["mcol"], writes=["T4"])
        S.op("pe", lambda e: e.transpose(out=PA[0][0:96, 0:128], in_=T[4][:, 0:96], identity=identf[:]), reads=["T4", "identf"], writes=["PA0"])
        S.op("dve", lambda e: e.tensor_copy(out=T[5][0:96, 0:128], in_=PA[0][0:96, 0:128]), reads=["PA0"], writes=["T5"])
        S.dma("sp", "l_modrow", lambda e: e.dma_start(out=modrow_d[s].rearrange("(j p) -> j p", p=128), in_=T[5][0:96, 0:128]),
              reads=["T5"], writes=["modrow"])

    def wsrc(blk):
        if blk < 112:
            return w_in[:, blk * 128:(blk + 1) * 128]
        if blk < 128:
            return w_out[:, (blk - 112) * 128:(blk - 111) * 128]
        return w_pq[:, (blk - 128) * 128:(blk - 127) * 128]

    cast_eng = ["dve", "act", "pool"]
    for blk in range(NBLK):
        t = T[4 + blk % 3]
        tk = "T%d" % (4 + blk % 3)
        s = blk % NW
        S.dma("sp", "l_" + tk, lambda e: e.dma_start(
            out=t[:].rearrange("p (k c) -> p k c", k=NCH),
            in_=wsrc(blk).rearrange("(k p) c -> p k c", p=128)), writes=[tk])
        ce = cast_eng[blk % 3]
        if ce == "act":
            S.op("act", lambda e: e.copy(out=Wsl[s][:].rearrange("p k c -> p (k c)"), in_=t[:]), reads=[tk], writes=["W%d" % s])
        else:
            S.op(ce, lambda e: e.tensor_copy(out=Wsl[s][:].rearrange("p k c -> p (k c)"), in_=t[:]), reads=[tk], writes=["W%d" % s])
        S.dma("sp", "l_W%d" % s, lambda e: e.dma_start(out=wbf_d[blk], in_=Wsl[s][:]), reads=["W%d" % s], writes=[("wbf", blk)])

    for i, (src, dst, nm) in enumerate([(wra_d, wra_b, "wra_b"), (wri_d, wri_b, "wri_b"), (pkT_d, pk_b, "pk_b")]):
        t = T[i]
        tk = "T%d" % i
        S.dma("sp", "l_" + tk, lambda e: e.dma_start(out=t[:].rearrange("p (k c) -> p k c", k=NCH), in_=src), writes=[tk])
        S.op("dve", lambda e: e.tensor_copy(out=dst[:].rearrange("p k c -> p (k c)"), in_=t[:]), reads=[tk], writes=[nm])
    for j in range(5):
        t = T[j % 4]
        tk = "T%d" % (j % 4)
        S.dma("sp", "l_" + tk, lambda e: e.dma_start(out=t[:].rearrange("p (h q) -> p h q", h=NCH), in_=BT_d[:, j]), writes=[tk])
        S.dma("sp", "l_mk", lambda e: e.dma_start(out=mk[:], in_=maskT_d[:, j]), writes=["mk"])
        S.op("dve", lambda e: e.tensor_tensor(out=BM[:, :, j, :], in0=T3[j % 4],
                                              in1=mk[:].unsqueeze(1).to_broadcast([128, NCH, 128]), op=ALU.add),
             reads=[tk, "mk"], writes=["BM"])
    S.op("dve", lambda e: e.memset(hst[:], 0.0), writes=["hst"])
    S.op("dve", lambda e: e.memset(halo[:], 0.0), writes=["halo"])
    S.barrier()

    def ln_stats(src, srck):
        for i in range(4):
            S.op("dve", lambda e: e.bn_stats(out=stt[:, i * 6:(i + 1) * 6], in_=src[:, i * 512:(i + 1) * 512]),
                 reads=[srck], writes=[("stt", i)])
        S.op("dve", lambda e: e.bn_aggr(out=mv[:], in_=stt[:]), reads=keys("stt", 4), writes=["mv"])
        S.op("act", lambda e: e.activation(out=sd[:], in_=mv[:, 1:2], func=AF.Sqrt, bias=cst[:, 4:5], scale=1.0),
             reads=["mv", "cst"], writes=["sd"])
        S.op("dve", lambda e: e.reciprocal(out=rstd[:], in_=sd[:]), reads=["sd"], writes=["rstd"])

    def transpose16(dstB, dstk, scale_bias=None):
        for g in range(2):
            for k8 in range(8):
                k = g * 8 + k8
                S.op("pe", lambda e: e.transpose(out=PT[:, k8 * 128:(k8 + 1) * 128], in_=Bx[:, k * 128:(k + 1) * 128],
                                                 identity=identb[:]), reads=["Bx", "identb"], writes=["PT"])
            ptv = PT[:].rearrange("p (k t) -> p k t", k=8)
            if scale_bias is None:
                S.op("act", lambda e: e.copy(out=dstB[:, g * 8:(g + 1) * 8, :], in_=ptv), reads=["PT"], writes=[(dstk, g)])
            else:
                sc, bi = scale_bias
                tmp = T[6][:, 0:1024].rearrange("p (k t) -> p k t", k=8)
                S.op("dve", lambda e: e.tensor_tensor(out=tmp, in0=ptv, in1=sc[:, g * 8:(g + 1) * 8].unsqueeze(2).to_broadcast([128, 8, 128]),
                                                      op=ALU.mult), reads=["PT", "mcol"], writes=["T6"])
                S.op("dve", lambda e: e.tensor_tensor(out=dstB[:, g * 8:(g + 1) * 8, :], in0=tmp,
                                                      in1=bi[:, g * 8:(g + 1) * 8].unsqueeze(2).to_broadcast([128, 8, 128]),
                                                      op=ALU.add), reads=["T6", "mcol"], writes=[(dstk, g)])

    def proj_fm(actB, actk, evac):
        for half in range(2):
            for n8 in range(8):
                w, wk = w_get()
                for k in range(NCH):
                    S.op("pe", lambda e: e.matmul(PA[half][:, n8 * 128:(n8 + 1) * 128], lhsT=w[:, k, :], rhs=actB[:, k, :],
                                                  start=(k == 0), stop=(k == NCH - 1)),
                         reads=[wk] + keys(actk, 2), writes=["PA%d" % half])
            evac(half, PA[half][:].rearrange("p (n t) -> p n t", n=8))

    def proj_tm(actB, actk, evac, extra=None):
        for half in range(2):
            for n8 in range(8):
                w, wk = w_get()
                for k in range(NCH):
                    S.op("pe", lambda e: e.matmul(PA[half][:, n8 * 128:(n8 + 1) * 128], lhsT=actB[:, k, :], rhs=w[:, k, :],
                                                  start=(k == 0), stop=(k == NCH - 1)),
                         reads=[wk] + keys(actk, 2), writes=["PA%d" % half])
                if extra is not None:
                    extra(half, n8, w, wk)
            evac(half, PA[half])

    def do_tile(x_ap, nvalid, seq, mode, rp, y_ap=None, kv_ap=None, halo_blocks=(), own_mask=False):
        full = (mode == "full")
        kv = mode in ("prekv", "full")
        mc = lambda lo: mcol[:, lo:lo + 16, seq]
        if nvalid < 128:
            S.op("dve", lambda e: e.memset(X[:], 0.0), writes=["X"])
        S.dma("sp", "l_X", lambda e: e.dma_start(out=X[0:nvalid, :], in_=x_ap), writes=["X"])
        ln_stats(X, "X")
        S.op("dve", lambda e: e.tensor_scalar(Bx[:], X[:], mv[:, 0:1], rstd[:, 0:1], ALU.subtract, ALU.mult),
             reads=["X", "mv", "rstd"], writes=["Bx"])
        transpose16(Bh, "Bh", scale_bias=(mc(16), mc(0)))

        S.op("dve", lambda e: e.tensor_copy(out=XE[:, :, 0:3], in_=halo[:]), reads=["halo"], writes=[("XE", "h")])

        def ev_xr(half, pv):
            S.op("act", lambda e: e.copy(out=XE[:, half * 8:(half + 1) * 8, 3:131], in_=pv), reads=["PA%d" % half], writes=[("XE", half)])
        proj_fm(Bh, "Bh", ev_xr)
        xek = ["XE"]
        cw = lambda j: bc16(colp[:, :, j])
        S.op("dve", lambda e: e.tensor_tensor(out=T3[0], in0=XE[:, :, 0:128], in1=cw(0), op=ALU.mult), reads=xek + ["colp"], writes=["T0"])
        for j in range(1, 4):
            S.op("dve", lambda e: e.tensor_tensor(out=T3[1], in0=XE[:, :, j:j + 128], in1=cw(j), op=ALU.mult), reads=xek + ["colp"], writes=["T1"])
            S.op("dve", lambda e: e.tensor_tensor(out=T[0][:], in0=T[0][:], in1=T[1][:], op=ALU.add), reads=["T0", "T1"], writes=["T0"])
        S.op("dve", lambda e: e.tensor_tensor(out=T3[0], in0=T3[0], in1=cw(4), op=ALU.add), reads=["T0", "colp"], writes=["T0"])
        S.op("dve", lambda e: e.tensor_copy(out=halo[:], in_=XE[:, :, nvalid:nvalid + 3]), reads=xek, writes=["halo"])
        S.op("act", lambda e: e.copy(out=Bx[:], in_=T[0][:]), reads=["T0"], writes=["Bx"])
        Bx3 = Bx[:].rearrange("p (n t) -> p n t", n=NCH)
        for gi, (wg, wgk, bcol, dst) in enumerate([(wra_b, "wra_b", 5, 1), (wri_b, "wri_b", 6, 2)]):
            for half in range(2):
                for n8 in range(8):
                    n = half * 8 + n8
                    S.op("pe", lambda e: e.matmul(PA[half][:, n8 * 128:(n8 + 1) * 128], lhsT=wg[:, n, :], rhs=Bx3[:, n, :],
                                                  start=True, stop=True), reads=[wgk, "Bx"], writes=["PA%d" % half])
                S.op("dve", lambda e: e.tensor_tensor(out=T3[dst][:, half * 8:(half + 1) * 8, :],
                                                      in0=PA[half][:].rearrange("p (n t) -> p n t", n=8),
                                                      in1=colp[:, half * 8:(half + 1) * 8, bcol].unsqueeze(2).to_broadcast([128, 8, 128]),
                                                      op=ALU.add), reads=["PA%d" % half, "colp"], writes=[("T%d" % dst, half)])
            S.op("act", lambda e: e.activation(out=T[dst][:], in_=T[dst][:], func=AF.Sigmoid), reads=keys("T%d" % dst, 2), writes=["T%d" % dst])
        S.op("dve", lambda e: e.tensor_tensor(out=T3[1], in0=T3[1], in1=bc16(coef[:]), op=ALU.mult), reads=["T1", "coef"], writes=["T1"])
        S.op("act", lambda e: e.activation(out=T[1][:], in_=T[1][:], func=AF.Exp), reads=["T1"], writes=["T1"])
        S.op("dve", lambda e: e.tensor_tensor(out=T[3][:], in0=T[1][:], in1=T[1][:], op=ALU.mult), reads=["T1"], writes=["T3"])
        S.op("dve", lambda e: e.tensor_scalar(T[3][:], T[3][:], -1.0, 1.0, ALU.mult, ALU.add), reads=["T3"], writes=["T3"])
        S.op("act", lambda e: e.activation(out=T[3][:], in_=T[3][:], func=AF.Sqrt), reads=["T3"], writes=["T3"])
        S.op("dve", lambda e: e.tensor_tensor(out=T[2][:], in0=T[2][:], in1=T[3][:], op=ALU.mult), reads=["T2", "T3"], writes=["T2"])
        S.op("dve", lambda e: e.tensor_tensor(out=T[2][:], in0=T[2][:], in1=T[0][:], op=ALU.mult), reads=["T2", "T0"], writes=["T2"])
        for n in range(NCH):
            S.op("dve", lambda e: e.tensor_tensor_scan(out=T[3][:, n * 128:(n + 1) * 128], data0=T[1][:, n * 128:(n + 1) * 128],
                                                       data1=T[2][:, n * 128:(n + 1) * 128], initial=hst[:, n:n + 1],
                                                       op0=ALU.mult, op1=ALU.add),
                 reads=["T1", "T2", "hst"], writes=[("T3", n)])
        S.op("dve", lambda e: e.tensor_copy(out=hst[:], in_=T3[3][:, :, nvalid - 1]), reads=keys("T3", NCH), writes=["hst"])
        if not kv:
            return
        slot = rp % 5
        blocks = [(rp - 4 + j) % 5 for j in range(5)]

        if full:
            def ev_yr(half, pv):
                S.op("act", lambda e: e.activation(out=T3[0][:, half * 8:(half + 1) * 8, :], in_=pv, func=AF.Gelu),
                     reads=["PA%d" % half], writes=[("T0", half)])
            proj_fm(Bh, "Bh", ev_yr)
            S.op("dve", lambda e: e.tensor_tensor(out=T[3][:], in0=T[3][:], in1=T[0][:], op=ALU.mult),
                 reads=keys("T3", NCH) + keys("T0", 2), writes=["T3"])

            def ev_ga(half, pv):
                S.op("act", lambda e: e.activation(out=T3[0][:, half * 8:(half + 1) * 8, :], in_=pv, func=AF.Sigmoid),
                     reads=["PA%d" % half], writes=[("T0", half)])
            proj_fm(Bh, "Bh", ev_ga)
            S.op("dve", lambda e: e.tensor_tensor(out=T[3][:], in0=T[3][:], in1=T[0][:], op=ALU.mult),
                 reads=["T3"] + keys("T0", 2), writes=["T3"])

            def ev_q(half, pv):
                S.op("act", lambda e: e.activation(out=Bq[:, half * 8:(half + 1) * 8, :], in_=pv, func=AF.Copy, scale=128.0 ** -0.5),
                     reads=["PA%d" % half], writes=[("Bq", half)])
            proj_fm(Bh, "Bh", ev_q)

        if full:
            _chk(31)
        def ev_k(half, pv):
            S.op("act", lambda e: e.copy(out=Kr[:, half * 8:(half + 1) * 8, slot, :], in_=pv), reads=["PA%d" % half], writes=[("Kr", slot)])
        if kv_ap is not None:
            for half in range(2):
                for n8 in range(8):
                    w, wk = w_get()
                    for k in range(NCH):
                        S.op("pe", lambda e: e.matmul(PA[half][:, n8 * 128:(n8 + 1) * 128], lhsT=w[:, k, :], rhs=Bh[:, k, :],
                                                      start=(k == 0), stop=(k == NCH - 1)),
                             reads=[wk] + keys("Bh", 2), writes=["PA%d" % half])
                    for k in range(NCH):
                        S.op("pe", lambda e: e.matmul(PSs[:, n8 * 128:(n8 + 1) * 128], lhsT=Bh[:, k, :], rhs=w[:, k, :],
                                                      start=(k == 0), stop=(k == NCH - 1)),
                             reads=[wk] + keys("Bh", 2), writes=["PSs"])
                ev_k(half, PA[half][:].rearrange("p (n t) -> p n t", n=8))
                S.op("dve", lambda e: e.tensor_copy(out=T[4][:, half * 1024:(half + 1) * 1024], in_=PSs[:]), reads=["PSs"], writes=[("T4", half)])
            S.dma("sp", "l_T4", lambda e: e.dma_start(out=kv_ap[0], in_=T[4][0:nvalid, :]), reads=keys("T4", 2), writes=["okv"])
        else:
            proj_fm(Bh, "Bh", ev_k)

        if full:
            _chk(32)
        def ev_v(half, pa):
            S.op("dve", lambda e: e.tensor_copy(out=Vr[:, slot, half * 1024:(half + 1) * 1024], in_=pa[:]), reads=["PA%d" % half], writes=[("Vr", slot)])
            if kv_ap is not None:
                S.op("dve", lambda e: e.tensor_copy(out=T[5][:, half * 1024:(half + 1) * 1024], in_=pa[:]), reads=["PA%d" % half], writes=[("T5", half)])
        proj_tm(Bh, "Bh", ev_v)
        if kv_ap is not None:
            S.dma("sp", "l_T5", lambda e: e.dma_start(out=kv_ap[1], in_=T[5][0:nvalid, :]), reads=keys("T5", 2), writes=["okv2"])
        if not full:
            return

        _chk(33)
        for h in range(NCH):
            for j in range(5):
                S.op("pe", lambda e: e.matmul(PSs[:, j * 128:(j + 1) * 128], lhsT=Kr[:, h, blocks[j], :], rhs=Bq[:, h, :],
                                              start=True, stop=True),
                     reads=[("Kr", blocks[j])] + keys("Bq", 2), writes=["PSs"])
            sx = T[1][:, (h % 3) * 640:(h % 3) * 640 + 640]
            sxk = ("T1", "s%d" % (h % 3))
            S.op("dve", lambda e: e.tensor_tensor(out=sx, in0=PSs[:, 0:640], in1=BM[:, h, :, :].rearrange("p j q -> p (j q)"), op=ALU.add),
                 reads=["PSs", "BM"], writes=[sxk])
            E = Eb[h % 2]
            ek = "E%d" % (h % 2)
            if halo_blocks or own_mask:
                for j in range(5):
                    bcol = 2
                    if j in halo_blocks:
                        bcol = 1
                    if own_mask and j == 4:
                        bcol = 3
                    S.op("act", lambda e: e.activation(out=E[:, j, :], in_=sx[:, j * 128:(j + 1) * 128], func=AF.Exp, bias=cst[:, bcol:bcol + 1]),
                         reads=[sxk, "cst"], writes=[(ek, j)])
                ekeys = keys(ek, 5)
            else:
                S.op("act", lambda e: e.activation(out=E[:].rearrange("p j q -> p (j q)"), in_=sx, func=AF.Exp), reads=[sxk], writes=[(ek, 0)])
                ekeys = [(ek, 0)]
            pvo = (h % 2) * 256
            pvk = ("PV", h % 2)
            for j in range(5):
                S.op("pe", lambda e: e.matmul(PV[:, pvo:pvo + 128], lhsT=Vr[:, blocks[j], h * 128:(h + 1) * 128], rhs=E[:, j, :],
                                              start=(j == 0), stop=(j == 4)), reads=[("Vr", blocks[j])] + ekeys, writes=[pvk])
            for j in range(5):
                S.op("pe", lambda e: e.matmul(PV[:, pvo + 128:pvo + 256], lhsT=onesb[:], rhs=E[:, j, :],
                                              start=(j == 0), stop=(j == 4)), reads=["onesb"] + ekeys, writes=[pvk])
            S.op("dve", lambda e: e.reciprocal(out=RZ[:], in_=PV[:, pvo + 128:pvo + 256]), reads=[pvk], writes=["RZ"])
            S.op("dve", lambda e: e.tensor_tensor(out=T[0][:, h * 128:(h + 1) * 128], in0=PV[:, pvo:pvo + 128], in1=RZ[:], op=ALU.mult),
                 reads=[pvk, "RZ"], writes=[("T0", "h%d" % h)])

        _chk(34)

        def ev_gb(half, pv):
            S.op("act", lambda e: e.activation(out=T3[2][:, half * 8:(half + 1) * 8, :], in_=pv, func=AF.Sigmoid),
                 reads=["PA%d" % half], writes=[("T2", half)])
        proj_fm(Bh, "Bh", ev_gb)
        S.op("dve", lambda e: e.tensor_tensor(out=T[0][:], in0=T[0][:], in1=T[2][:], op=ALU.mult),
             reads=["T0", "T2"], writes=["T0"])
        S.op("dve", lambda e: e.tensor_tensor(out=Bm[:].rearrange("p n t -> p (n t)"), in0=T[0][:], in1=T[3][:], op=ALU.add),
             reads=["T0", "T3"], writes=["Bm"])

        _chk(3)
        C, ck = c_get()

        def ev_m(half, pa):
            S.op("dve", lambda e: e.tensor_tensor(out=T[5][:, half * 1024:(half + 1) * 1024], in0=pa[:], in1=C[:, half * 1024:(half + 1) * 1024], op=ALU.mult),
                 reads=["PA%d" % half, ck], writes=[("T5", half)])
        proj_tm(Bm, "Bm", ev_m)
        S.op("dve", lambda e: e.scalar_tensor_tensor(out=T[6][:], in0=X[:], scalar=ALPHA, in1=T[5][:], op0=ALU.mult, op1=ALU.add),
             reads=["X", "T5"], writes=["T6"])
        ln_stats(T[6], "T6")
        S.op("dve", lambda e: e.tensor_scalar(T[6][:], T[6][:], mv[:, 0:1], rstd[:, 0:1], ALU.subtract, ALU.mult),
             reads=["T6", "mv", "rstd"], writes=["T6"])
        C, ck = c_get()
        S.op("dve", lambda e: e.tensor_tensor(out=T[6][:], in0=T[6][:], in1=C[:], op=ALU.mult), reads=["T6", ck], writes=["T6"])
        C, ck = c_get()
        S.op("dve", lambda e: e.tensor_tensor(out=T[6][:], in0=T[6][:], in1=C[:], op=ALU.add), reads=["T6", ck], writes=["T6"])
        ln_stats(T[6], "T6")
        S.op("dve", lambda e: e.tensor_scalar(T[0][:], T[6][:], mv[:, 0:1], rstd[:, 0:1], ALU.subtract, ALU.mult),
             reads=["T6", "mv", "rstd"], writes=["T0"])
        C, ck = c_get()
        S.op("dve", lambda e: e.tensor_tensor(out=T[0][:], in0=T[0][:], in1=C[:], op=ALU.mult), reads=["T0", ck], writes=["T0"])
        C, ck = c_get()
        S.op("dve", lambda e: e.tensor_tensor(out=T[0][:], in0=T[0][:], in1=C[:], op=ALU.add), reads=["T0", ck], writes=["T0"])
        S.op("act", lambda e: e.copy(out=Bx[:], in_=T[0][:]), reads=["T0"], writes=["Bx"])
        transpose16(Bh, "Bh")

        _chk(4)
        def ev_qp(half, pv):
            S.op("act", lambda e: e.copy(out=Bq[:, half * 8:(half + 1) * 8, :], in_=pv), reads=["PA%d" % half], writes=[("Bq", half)])
        proj_fm(Bh, "Bh", ev_qp)
        for half in range(2):
            for n8 in range(8):
                j = half * 8 + n8
                S.op("pe", lambda e: e.matmul(PA[half][:, n8 * 128:(n8 + 1) * 128], lhsT=Bq[:, j, :], rhs=pk_b[:, j, :], start=True, stop=True),
                     reads=["pk_b"] + keys("Bq", 2), writes=["PA%d" % half])
            S.op("dve", lambda e: e.tensor_copy(out=T[2][:, half * 1024:(half + 1) * 1024], in_=PA[half][:]), reads=["PA%d" % half], writes=[("T2", half)])
        for j in range(NCH):
            sl = T[2][:, j * 128:(j + 1) * 128]
            k2 = ("T2", "j%d" % j)
            rd = keys("T2", 2)
            S.op("dve", lambda e: e.max(out=st16[:, j, 0:8], in_=sl), reads=rd + [k2], writes=[("st16", j)])
            S.op("dve", lambda e: e.max_index(out=ix16[:, j, 0:8], in_max=st16[:, j, 0:8], in_values=sl), reads=rd + [k2, ("st16", j)], writes=[("ix16", j)])
            S.op("dve", lambda e: e.match_replace(out=sl, in_to_replace=st16[:, j, 0:8], in_values=sl, imm_value=NEG), reads=rd + [("st16", j)], writes=[k2])
            S.op("dve", lambda e: e.max(out=st16[:, j, 8:16], in_=sl), reads=[k2], writes=[("st16", j)])
            S.op("dve", lambda e: e.max_index(out=ix16[:, j, 8:16], in_max=st16[:, j, 8:16], in_values=sl), reads=[k2, ("st16", j)], writes=[("ix16", j)])
        S.op("dve", lambda e: e.tensor_copy(out=ixf[:], in_=ix16[:]), reads=keys("ix16", NCH), writes=["ixf"])
        st4 = st16[:].rearrange("p (h two) k -> p h two k", two=2)
        ix4 = ixf[:].rearrange("p (h two) k -> p h two k", two=2)
        S.op("dve", lambda e: e.tensor_scalar(i1s[:], ix4[:, :, 0, :], 128.0, None, ALU.mult), reads=["ixf"], writes=["i1s"])
        c4 = lambda t: t[:].rearrange("p (h a b) -> p h a b", h=8, a=16)
        S.op("dve", lambda e: e.tensor_tensor(out=c4(T[3]), in0=st4[:, :, 0, :].unsqueeze(3).to_broadcast([128, 8, 16, 16]),
                                              in1=st4[:, :, 1, :].unsqueeze(2).to_broadcast([128, 8, 16, 16]), op=ALU.add),
             reads=keys("st16", NCH), writes=["T3"])
        S.op("dve", lambda e: e.tensor_tensor(out=c4(T[4]), in0=i1s[:].unsqueeze(3).to_broadcast([128, 8, 16, 16]),
                                              in1=ix4[:, :, 1, :].unsqueeze(2).to_broadcast([128, 8, 16, 16]), op=ALU.add),
             reads=["i1s", "ixf"], writes=["T4"])
        for h in range(8):
            cs = T[3][:, h * 256:(h + 1) * 256]
            c5 = T[5][:, h * 256:(h + 1) * 256]
            S.op("dve", lambda e: e.max(out=tops[:, h, 0:8], in_=cs), reads=["T3"], writes=[("tops", h)])
            S.op("dve", lambda e: e.match_replace(out=c5, in_to_replace=tops[:, h, 0:8], in_values=cs, imm_value=NEG),
                 reads=["T3", ("tops", h)], writes=[("T5", "c%d" % h)])
            S.op("dve", lambda e: e.max(out=tops[:, h, 8:16], in_=c5), reads=[("T5", "c%d" % h)], writes=[("tops", h)])
            for k in range(16):
                S.op("dve", lambda e: e.scalar_tensor_tensor(out=c5, in0=cs, scalar=tops[:, h, k:k + 1], in1=T[4][:, h * 256:(h + 1) * 256],
                                                             op0=ALU.is_equal, op1=ALU.mult, accum_out=eidf[:, h * 16 + k:h * 16 + k + 1]),
                     reads=["T3", "T4", ("tops", h)], writes=[("T5", "c%d" % h), ("eidf", h * 16 + k)])
        S.op("dve", lambda e: e.tensor_copy(out=eid[:], in_=eidf[:]), reads=keys("eidf", 128), writes=["eid"])
        S.op("dve", lambda e: e.tensor_tensor(out=gw[:], in0=tops[:], in1=tops[:, :, 0:1].to_broadcast([128, 8, 16]), op=ALU.subtract),
             reads=keys("tops", 8), writes=["gw"])
        S.op("act", lambda e: e.activation(out=gw[:], in_=gw[:], func=AF.Exp), reads=["gw"], writes=["gw"])
        S.op("dve", lambda e: e.tensor_reduce(out=gs[:], in_=gw[:], axis=AX.X, op=ALU.add), reads=["gw"], writes=["gs"])
        S.op("dve", lambda e: e.reciprocal(out=gs[:], in_=gs[:]), reads=["gs"], writes=["gs"])
        S.op("dve", lambda e: e.tensor_tensor(out=gw[:], in0=gw[:], in1=gs[:].unsqueeze(2).to_broadcast([128, 8, 16]), op=ALU.mult),
             reads=["gw", "gs"], writes=["gw"])
        _chk(5)
        NG = 4
        for j in range(128):
            g = T[2 + j % NG]
            gk = "T%d" % (2 + j % NG)
            S.dma("pool", "l_" + gk, lambda e: e.indirect_dma_start(out=g[:], out_offset=None, in_=peer_u,
                  in_offset=bass.IndirectOffsetOnAxis(ap=eid[:, j:j + 1], axis=0), bounds_check=bc_reg, oob_is_err=False),
                  reads=["eid"],
                  writes=[gk])
            S.op("dve", lambda e: e.scalar_tensor_tensor(out=Bx[:], in0=g[:], scalar=1.0, in1=T[0][:], op0=ALU.mult, op1=ALU.mult,
                                                         accum_out=pre[:, j:j + 1]), reads=[gk, "T0"], writes=["Bx", ("pre", j)])
        S.op("act", lambda e: e.activation(out=cf[:], in_=pre[:], func=AF.Gelu), reads=keys("pre", 128), writes=["cf"])
        S.op("dve", lambda e: e.tensor_tensor(out=cf[:], in0=cf[:], in1=gw[:].rearrange("p h k -> p (h k)"), op=ALU.mult), reads=["cf", "gw"], writes=["cf"])
        for j in range(128):
            g = T[2 + j % NG]
            gk = "T%d" % (2 + j % NG)
            S.dma("pool", "l_" + gk, lambda e: e.indirect_dma_start(out=g[:], out_offset=None, in_=peer_v,
                  in_offset=bass.IndirectOffsetOnAxis(ap=eid[:, j:j + 1], axis=0), bounds_check=bc_reg, oob_is_err=False),
                  reads=["eid"], writes=[gk])
            if j == 0:
                S.op("dve", lambda e: e.tensor_scalar(T[1][:], g[:], cf[:, 0:1], None, ALU.mult), reads=[gk, "cf"], writes=["T1"])
            else:
                S.op("dve", lambda e: e.scalar_tensor_tensor(out=T[1][:], in0=g[:], scalar=cf[:, j:j + 1], in1=T[1][:], op0=ALU.mult, op1=ALU.add),
                     reads=[gk, "cf", "T1"], writes=["T1"])
        _chk(6)
        C, ck = c_get()
        S.op("dve", lambda e: e.tensor_tensor(out=T[1][:], in0=T[1][:], in1=C[:], op=ALU.mult), reads=["T1", ck], writes=["T1"])
        S.op("dve", lambda e: e.scalar_tensor_tensor(out=T[1][:], in0=T[6][:], scalar=ALPHA, in1=T[1][:], op0=ALU.mult, op1=ALU.add),
             reads=["T6", "T1"], writes=["T1"])
        ln_stats(T[1], "T1")
        S.op("dve", lambda e: e.tensor_scalar(T[1][:], T[1][:], mv[:, 0:1], rstd[:, 0:1], ALU.subtract, ALU.mult),
             reads=["T1", "mv", "rstd"], writes=["T1"])
        C, ck = c_get()
        S.op("dve", lambda e: e.tensor_tensor(out=T[1][:], in0=T[1][:], in1=C[:], op=ALU.mult), reads=["T1", ck], writes=["T1"])
        C, ck = c_get()
        S.op("dve", lambda e: e.tensor_tensor(out=T[1][:], in0=T[1][:], in1=C[:], op=ALU.add), reads=["T1", ck], writes=["T1"])
        S.dma("sp", "l_T1", lambda e: e.dma_start(out=y_ap, in_=T[1][0:nvalid, :]), reads=["T1"], writes=["yout"])

    XR = list(range(0, 16)); YR = list(range(16, 32)); QQ = list(range(32, 48)); KK = list(range(48, 64))
    VV = list(range(64, 80)); GA = list(range(80, 96)); GB = list(range(96, 112)); WO = list(range(112, 128)); WP = list(range(128, 144))
    order = []
    for t in range(NT_PRE):
        order += XR
        if t >= NT_PRE - 4:
            order += KK + VV
    for t in range(NT_MAIN + 1):
        order += XR + YR + GA + QQ + KK + VV + GB + WO + WP
    wstate["order"] = order
    mrow = lambda s, v: modrow_d[s:s + 1, v * D:(v + 1) * D]
    corder = []
    for t in range(NT_MAIN + 1):
        s = 0 if t < NT_MAIN else 1
        corder += [mrow(s, 2), ln1g, ln1b, mrow(s, 4), mrow(s, 3), mrow(s, 5), ln2g, ln2b]
    cstate["order"] = corder

    try:
        _chk(1)
        rp = 0
        for t in range(NT_PRE):
            if t >= NT_PRE - 4:
                do_tile(x_pre[t * 128:(t + 1) * 128, :], 128, 0, "prekv", rp)
                rp += 1
            else:
                do_tile(x_pre[t * 128:(t + 1) * 128, :], 128, 0, "pre", rp)
        _chk(2)
        S.op("dve", lambda e: e.tensor_scalar(hst[:], hst[:], cst[:, 0:1], None, ALU.mult), reads=["hst", "cst"], writes=["hst"])
        S.op("dve", lambda e: e.tensor_scalar(halo[:].rearrange("p n t -> p (n t)"), halo[:].rearrange("p n t -> p (n t)"), cst[:, 0:1], None, ALU.mult),
             reads=["halo", "cst"], writes=["halo"])
        for t in range(NT_MAIN):
            hb = tuple(j for j in range(5) if (t - 4 + j) < 0)
            kvo = None
            if t >= NT_MAIN - NKV:
                r0 = (t - (NT_MAIN - NKV)) * 128
                kvo = (o_k[r0:r0 + 128, :], o_v[r0:r0 + 128, :])
            do_tile(x_main[t * 128:(t + 1) * 128, :], 128, 0, "full", rp, y_ap=y_main[t * 128:(t + 1) * 128, :], kv_ap=kvo, halo_blocks=hb)
            rp += 1
        def state_out(conv_ap, rnn_ap):
            with nc.allow_non_contiguous_dma(reason="tiny state outputs"):
                for n in range(NCH):
                    S.dma("sp", "l_oc", lambda e: e.dma_start(out=conv_ap[:, n * 128:(n + 1) * 128].rearrange("t p -> p t"), in_=halo[:, n, :]),
                          reads=["halo"], writes=["oc"])
                    S.dma("sp", "l_or", lambda e: e.dma_start(out=rnn_ap[n * 128:(n + 1) * 128].rearrange("(p o) -> p o", o=1), in_=hst[:, n:n + 1]),
                          reads=["hst"], writes=["or"])

        state_out(o_conv, o_rnn)
        with nc.allow_non_contiguous_dma(reason="tiny state inputs"):
            for n in range(NCH):
                S.dma("sp", "l_halo", lambda e: e.dma_start(out=halo[:, n, :], in_=st_conv[:, n * 128:(n + 1) * 128].rearrange("t p -> p t")),
                      reads=["oc"], writes=["halo"])
                S.dma("sp", "l_hst", lambda e: e.dma_start(out=hst[:, n:n + 1], in_=st_rnn[n * 128:(n + 1) * 128].rearrange("(p o) -> p o", o=1)),
                      reads=["or"], writes=["hst"])
        for j in range(4):
            sl = (rp - 4 + j) % 5
            S.dma("pool", "l_ck", lambda e: e.dma_start(out=Kr[:, :, sl, :], in_=ckT[:, :, j * 128:(j + 1) * 128]), writes=[("Kr", sl)])
            S.dma("pool", "l_cv", lambda e: e.dma_start(out=Vr[:, sl, :], in_=cv[j * 128:(j + 1) * 128, :]), writes=[("Vr", sl)])
        do_tile(x_smp, 32, 1, "full", rp, y_ap=y_smp, kv_ap=(s_k, s_v), own_mask=True)
        state_out(s_conv, s_rnn)
    except _Stop:
        pass
    S.finish("sp")
    es.close()
    S.close()
    return nc


def _prep_shared(inp):
    f = lambda a: np.ascontiguousarray(a, dtype=np.float32)
    sh = {}
    sh["w_ada"] = f(inp["w_ada"][0])
    sh["badaT"] = f(inp["b_ada"][0].reshape(96, 128).T)
    sh["w_in"] = f(inp["w_in"][0])
    vecs = [inp["conv_w"][0][0], inp["conv_w"][0][1], inp["conv_w"][0][2], inp["conv_w"][0][3],
            inp["conv_b"][0], inp["b_ra"][0], inp["b_ri"][0], inp["rg_lambda"][0]]
    sh["colp"] = f(np.stack([np.asarray(v).reshape(NCH, 128).T for v in vecs], axis=-1))
    sh["wra"] = f(np.transpose(inp["w_ra"][0], (1, 0, 2)))
    sh["wri"] = f(np.transpose(inp["w_ri"][0], (1, 0, 2)))
    p = np.arange(128)[:, None, None]
    j = np.arange(5)[None, :, None]
    q = np.arange(128)[None, None, :]
    kpos = -512 + 128 * j + p
    idx = np.clip(q - kpos, -63, 128) + 63
    rb = np.asarray(inp["rel_bias"][0])
    sh["BT"] = f(np.transpose(rb[:, idx], (1, 2, 0, 3)))
    qc = q // 64
    kc = np.floor_divide(kpos, 64)
    vis = (kc <= qc) & (kc >= qc - 8)
    sh["maskT"] = f(np.where(vis, 0.0, NEG))
    sh["w_out"] = f(inp["w_out"][0])
    sh["ln1g"] = f(inp["ln1_g"][0][None, :])
    sh["ln1b"] = f(inp["ln1_b"][0][None, :])
    sh["w_pq"] = f(inp["w_pq"][0])
    pk = np.asarray(inp["peer_keys"][0]).reshape(16, 128, 128)
    sh["pkT"] = f(np.transpose(pk, (2, 0, 1)))
    sh["peer_u"] = f(inp["peer_u"][0])
    sh["peer_v"] = f(inp["peer_v"][0])
    sh["ln2g"] = f(inp["ln2_g"][0][None, :])
    sh["ln2b"] = f(inp["ln2_b"][0][None, :])
    return sh


def kernel(**inp):
    inp = {k: np.asarray(v) for k, v in inp.items()}
    B, SEQ, _ = inp["x_prompt"].shape
    half_len = SEQ // 2
    NT = half_len // 128
    nc = build_nc(NT, NT)
    sh = _prep_shared(inp)
    f = lambda a: np.ascontiguousarray(a, dtype=np.float32)
    in_maps = []
    NCORES = int(os.environ.get('KCORES', '8'))
    for c in range(NCORES):
        b, hf = c // 2, c % 2
        m = dict(sh)
        m["x_main"] = f(inp["x_prompt"][b, hf * half_len:(hf + 1) * half_len])
        m["x_pre"] = f(inp["x_prompt"][b, 0:half_len])
        m["x_smp"] = f(inp["x_sample"][c])
        cvec = np.stack([inp["c_prompt"][b], inp["c_sample"][c]], axis=0)
        m["cT"] = f(np.transpose(cvec.reshape(2, NCH, 128), (2, 1, 0)))
        cst = np.zeros((128, 8), np.float32)
        cst[:, 0] = float(hf)
        cst[:, 1] = 0.0 if hf == 1 else NEG
        cst[32:, 3] = NEG
        cst[:, 4] = LN_EPS
        cst[:, 5] = 1.0
        m["cst"] = cst
        m["st_conv"] = f(inp["state_conv"][0, c])
        m["st_rnn"] = f(inp["state_rnn"][0, c])
        m["ckT"] = f(np.transpose(inp["cache_k"][0, c], (2, 1, 0)))
        m["cv"] = f(inp["cache_v"][0, c].reshape(512, D))
        in_maps.append(m)
    res = run_bass_kernel_spmd(nc, in_maps, core_ids=list(range(NCORES)))
    R = res.results
    y_prompt = np.zeros((B, SEQ, D), np.float32)
    y_sample = np.zeros((8, 32, D), np.float32)
    keep = min(512, SEQ)
    p_conv = np.zeros((1, B, 3, D), np.float32)
    p_rnn = np.zeros((1, B, D), np.float32)
    p_k = np.zeros((1, B, keep, 16, 128), np.float32)
    p_v = np.zeros((1, B, keep, 16, 128), np.float32)
    s_conv = np.zeros((1, 8, 3, D), np.float32)
    s_rnn = np.zeros((1, 8, D), np.float32)
    s_k = np.zeros((1, 8, 32, 16, 128), np.float32)
    s_v = np.zeros((1, 8, 32, 16, 128), np.float32)
    for c in range(NCORES):
        b, hf = c // 2, c % 2
        r = R[c]
        y_prompt[b, hf * half_len:(hf + 1) * half_len] = r["y_main"]
        y_sample[c] = r["y_smp"]
        if hf == 1:
            p_conv[0, b] = r["o_conv"]
            p_rnn[0, b] = r["o_rnn"]
            p_k[0, b] = r["o_k"].reshape(512, 16, 128)[-keep:]
            p_v[0, b] = r["o_v"].reshape(512, 16, 128)[-keep:]
        s_conv[0, c] = r["s_conv"]
        s_rnn[0, c] = r["s_rnn"]
        s_k[0, c] = r["s_k"].reshape(32, 16, 128)
        s_v[0, c] = r["s_v"].reshape(32, 16, 128)
    return (y_prompt, y_sample, p_conv, p_rnn, p_k, p_v, s_conv, s_rnn, s_k, s_v)
```
